# Optimizing a Trainium2 kernel written in Bass

```python
import jax, jax.numpy as jnp
from jax import lax
import numpy as np

D_MODEL = 4096
BATCH = 16
SEQ = 256
DEPTH = 4
DEC_BATCH = 4
DEC_SEQ = 4096
PAST_LEN = 256

GRID_W = 64
D_HG = D_MODEL // 2
HG_HEADS = 16
HG_DK = D_HG // HG_HEADS
HG_DV = D_HG // HG_HEADS
D_CV = D_MODEL // 2
CONV_W = 31
CHUNK = 32
EPS = 1e-6

IN_SIZES = (D_HG, D_HG, D_HG, D_HG, D_HG, D_CV, D_CV, D_CV, D_MODEL, D_MODEL)
IN_SPLITS = tuple(int(v) for v in np.cumsum(IN_SIZES)[:-1])
N_IN = int(sum(IN_SIZES))

kernel_name = "hgrn2_conformer_adaln_prefix_trunk"

F32 = jnp.float32


def _rmsnorm(x, g):
    x32 = x.astype(F32)
    y = x32 * lax.rsqrt(jnp.mean(x32 * x32, axis=-1, keepdims=True) + EPS)
    return (y * g.astype(F32)).astype(x.dtype)


def _layernorm(x, g, b):
    x32 = x.astype(F32)
    mu = jnp.mean(x32, axis=-1, keepdims=True)
    xc = x32 - mu
    y = xc * lax.rsqrt(jnp.mean(xc * xc, axis=-1, keepdims=True) + EPS)
    return (y * g.astype(F32) + b.astype(F32)).astype(x.dtype)


def _lower_bounds(lb_logits):
    p = jax.nn.softmax(lb_logits.astype(F32), axis=0)
    return jnp.cumsum(p, axis=0) - p[0:1]


def _chunk_scan(q, k, v, logf, s0):
    B, L, H, K = q.shape
    V = v.shape[-1]
    n = L // CHUNK

    def to_chunks(a):
        return a.astype(F32).reshape(B, n, CHUNK, H, a.shape[-1]).transpose(1, 0, 3, 2, 4)

    qs, ks, vs, gs = to_chunks(q), to_chunks(k), to_chunks(v), to_chunks(logf)
    causal = jnp.tril(jnp.ones((CHUNK, CHUNK), dtype=bool))

    def step(S, inp):
        qc, kc, vc, gc = inp
        b = jnp.cumsum(gc, axis=2)
        o_inter = jnp.einsum('bhtk,bhkv->bhtv', qc * jnp.exp(b), S)
        diff = b[:, :, :, None, :] - b[:, :, None, :, :]
        dec = jnp.exp(jnp.where(causal[:, :, None], diff, -jnp.inf))
        scores = jnp.einsum('bhtk,bhtsk,bhsk->bhts', qc, dec, kc)
        o_intra = jnp.einsum('bhts,bhsv->bhtv', scores, vc)
        b_end = b[:, :, -1, :]
        S_new = jnp.exp(b_end)[..., None] * S + jnp.einsum(
            'bhsk,bhsv->bhkv', kc * jnp.exp(b_end[:, :, None, :] - b), vc)
        return S_new, o_inter + o_intra

    S_fin, o = lax.scan(step, s0.astype(F32), (qs, ks, vs, gs))
    o = o.transpose(1, 0, 3, 2, 4).reshape(B, L, H, V)
    return o, S_fin


def _hgrn2(q_raw, ff_raw, fb_raw, i_raw, g_raw, lb, norm_g, s0_f, s0_b):
    B, L, _ = q_raw.shape

    def heads(a):
        return a.reshape(B, L, HG_HEADS, -1)

    q = heads(jax.nn.silu(q_raw.astype(F32)))
    v = heads(i_raw.astype(F32))

    def gates(z, lbd):
        z = z.astype(F32)
        logf = jnp.logaddexp(jnp.log(lbd), jnp.log1p(-lbd) + jax.nn.log_sigmoid(z))
        k = (1.0 - lbd) * jax.nn.sigmoid(-z)
        return heads(k), heads(logf)

    kf, gf = gates(ff_raw, lb[0])
    kb, gb = gates(fb_raw, lb[1])
    o_f, S_f = _chunk_scan(q, kf, v, gf, s0_f)
    o_b, S_b = _chunk_scan(q[:, ::-1], kb[:, ::-1], v[:, ::-1], gb[:, ::-1], s0_b)
    o = o_f + o_b[:, ::-1]
    o = o * lax.rsqrt(jnp.mean(o * o, axis=-1, keepdims=True) + EPS) * norm_g.astype(F32)
    o = o.reshape(B, L, D_HG) * jax.nn.silu(g_raw.astype(F32))
    return o.astype(q_raw.dtype), S_f, S_b


def _dwconv(u, w, b):
    y = lax.conv_general_dilated(
        u, w[:, None, :].astype(u.dtype), window_strides=(1,),
        padding=[(CONV_W // 2, CONV_W // 2)],
        dimension_numbers=('NWC', 'WIO', 'NWC'),
        feature_group_count=u.shape[-1])
    return y + b.astype(u.dtype)


def _conformer_conv(a, b_glu, g_raw, w_dw, b_dw, ln_g, ln_b, grid_mode):
    u = a * jax.nn.sigmoid(b_glu)
    B, L, C = u.shape
    if grid_mode is None:
        y = _dwconv(u, w_dw, b_dw)
    else:
        rows = L // GRID_W
        grid = u.reshape(B, rows, GRID_W, C)
        if grid_mode == 'row':
            y = _dwconv(grid.reshape(B * rows, GRID_W, C), w_dw, b_dw).reshape(B, L, C)
        else:
            lines = grid.transpose(0, 2, 1, 3).reshape(B * GRID_W, rows, C)
            y = _dwconv(lines, w_dw, b_dw).reshape(B, GRID_W, rows, C)
            y = y.transpose(0, 2, 1, 3).reshape(B, L, C)
    y = _layernorm(y, ln_g, ln_b)
    return jax.nn.silu(y) * jax.nn.silu(g_raw)


def _layer(x, mod, s0_f, s0_b, lb, grid_mode, norm_g, w_in, hg_norm_g, w_dw, b_dw,
           ln_g, ln_b, w_hproj, w_cproj, w_out):
    shift, scale, gate = jnp.split(mod.astype(x.dtype), 3, axis=-1)
    h = _rmsnorm(x, norm_g) * (1 + scale[:, None]) + shift[:, None]
    p = jnp.einsum('bld,dn->bln', h, w_in)
    q_raw, ff, fb, i_raw, g_hg, cv_a, cv_b, g_cv, m_hg, m_cv = jnp.split(p, IN_SPLITS, axis=-1)
    o_hg, S_f, S_b = _hgrn2(q_raw, ff, fb, i_raw, g_hg, lb, hg_norm_g, s0_f, s0_b)
    o_cv = _conformer_conv(cv_a, cv_b, g_cv, w_dw, b_dw, ln_g, ln_b, grid_mode)
    y_hg = jnp.einsum('blc,cd->bld', o_hg, w_hproj)
    y_cv = jnp.einsum('blc,cd->bld', o_cv, w_cproj)
    merged = jax.nn.sigmoid(m_hg) * y_hg + jax.nn.sigmoid(m_cv) * y_cv
    out = jnp.einsum('bld,de->ble', merged, w_out)
    return x + gate[:, None] * out, S_f, S_b


def setup_inputs(seed: int = 0) -> dict:
    key = jax.random.key(seed)
    ks = jax.random.split(key, 20)
    nrm = jax.random.normal
    D = D_MODEL
    return {
        "x_prompt": nrm(ks[0], (BATCH, SEQ, D), F32),
        "x_sample": nrm(ks[1], (DEC_BATCH, DEC_SEQ, D), F32),
        "state_hgrn": 0.5 * nrm(ks[2], (DEC_BATCH, DEPTH, 2, HG_HEADS, HG_DK, HG_DV), F32),
        "c": nrm(ks[3], (DEC_BATCH, D), F32),
        "c_ctx": nrm(ks[4], (D,), F32),
        "mod_w": 0.5 * D ** -0.5 * nrm(ks[5], (DEPTH, D, 3 * D), F32),
        "mod_b": 0.02 * nrm(ks[6], (DEPTH, 3 * D), F32),
        "norm_g": 1.0 + 0.05 * nrm(ks[7], (DEPTH, D), F32),
        "w_in": D ** -0.5 * nrm(ks[8], (DEPTH, D, N_IN), F32),
        "hg_lb_logits": nrm(ks[9], (DEPTH, 2, D_HG), F32),
        "hg_norm_g": 1.0 + 0.05 * nrm(ks[10], (DEPTH, HG_DV), F32),
        "cv_dw_w": CONV_W ** -0.5 * nrm(ks[11], (DEPTH, CONV_W, D_CV), F32),
        "cv_dw_b": 0.02 * nrm(ks[12], (DEPTH, D_CV), F32),
        "cv_ln_g": 1.0 + 0.05 * nrm(ks[13], (DEPTH, D_CV), F32),
        "cv_ln_b": 0.02 * nrm(ks[14], (DEPTH, D_CV), F32),
        "w_hproj": D_HG ** -0.5 * nrm(ks[15], (DEPTH, D_HG, D), F32),
        "w_cproj": D_CV ** -0.5 * nrm(ks[16], (DEPTH, D_CV, D), F32),
        "w_out": D ** -0.5 * nrm(ks[17], (DEPTH, D, D), F32),
        "final_g": 1.0 + 0.05 * nrm(ks[18], (D,), F32),
    }


def reference(x_prompt, x_sample, state_hgrn, c, c_ctx, mod_w, mod_b, norm_g, w_in,
              hg_lb_logits, hg_norm_g, cv_dw_w, cv_dw_b, cv_ln_g, cv_ln_b,
              w_hproj, w_cproj, w_out, final_g):
    lbs = _lower_bounds(hg_lb_logits)
    silu_ctx = jax.nn.silu(c_ctx.astype(F32))
    silu_c = jax.nn.silu(c.astype(F32))

    xp = x_prompt
    zeros = jnp.zeros((x_prompt.shape[0], HG_HEADS, HG_DK, HG_DV), F32)
    states = []
    for l in range(DEPTH):
        mod_ctx = (silu_ctx @ mod_w[l].astype(F32) + mod_b[l].astype(F32))[None]
        xp, S_f, S_b = _layer(xp, mod_ctx, zeros, zeros, lbs[l], None, norm_g[l], w_in[l],
                              hg_norm_g[l], cv_dw_w[l], cv_dw_b[l], cv_ln_g[l], cv_ln_b[l],
                              w_hproj[l], w_cproj[l], w_out[l])
        states.append(jnp.stack([S_f, S_b], axis=1))
    new_state_hgrn = jnp.stack(states, axis=1)

    xs = x_sample
    for l in range(DEPTH):
        mod = silu_c @ mod_w[l].astype(F32) + mod_b[l].astype(F32)
        grid_mode = 'row' if l % 2 == 0 else 'col'
        xs, _, _ = _layer(xs, mod, state_hgrn[:, l, 0], state_hgrn[:, l, 1], lbs[l], grid_mode,
                          norm_g[l], w_in[l], hg_norm_g[l], cv_dw_w[l], cv_dw_b[l],
                          cv_ln_g[l], cv_ln_b[l], w_hproj[l], w_cproj[l], w_out[l])

    y_prompt = _rmsnorm(xp, final_g)
    y_sample = _rmsnorm(xs, final_g)
    return (y_prompt, y_sample, new_state_hgrn)
```

```python
from contextlib import ExitStack

import numpy as np
import concourse.bass as bass
import concourse.mybir as mybir
from concourse.bass_utils import run_bass_kernel_spmd

F32 = mybir.dt.float32
BF16 = mybir.dt.bfloat16
AF = mybir.ActivationFunctionType
ALU = mybir.AluOpType

D = 4096
KC = 32
TS = 4096
TPR = 256
NPR = 2
T = TS + NPR * TPR
NBLK = T // 512
DH = 2048
H = 16
NL = 4
NIN = 24576
EPS = 1e-6
CW = 31
PCS = 2
NPC = KC // PCS
ENGS = ("pe", "act", "dve", "pool", "sp")


class Buf:
    __slots__ = ("name", "writer", "readers", "lane")

    def __init__(self, name, lane=None):
        self.name = name
        self.writer = None
        self.readers = []
        self.lane = lane


class Tile:
    __slots__ = ("ap", "b")

    def __init__(self, ap, b):
        self.ap = ap
        self.b = b


class Ring:
    def __init__(self, tiles):
        self.tiles = tiles
        self.i = 0

    def next(self):
        t = self.tiles[self.i % len(self.tiles)]
        self.i += 1
        return t


class Arena:
    def __init__(self, ap):
        self.ap = ap
        self.n = ap.shape[1]
        self.off = 0

    def reset(self):
        self.off = 0

    def f32(self, n):
        a = self.ap[:, self.off:self.off + n]
        self.off += n
        assert self.off <= self.n, ("arena overflow", self.off, self.n)
        return a

    def bf16(self, n):
        m = (n + 1) // 2
        a = self.ap[:, self.off:self.off + m].bitcast(BF16)
        self.off += m
        assert self.off <= self.n, ("arena overflow", self.off, self.n)
        return a


class Ctx:
    def __init__(self, nc, es, nlanes=48):
        self.nc = nc
        self.es = es
        self.eng = {"pe": nc.tensor, "act": nc.scalar, "dve": nc.vector, "pool": nc.gpsimd, "sp": nc.sync}
        self.sems = {}
        self.cnt = {}
        self.seen = {e: {} for e in ENGS}
        self.eng_semkey = {}
        self.gen = 0
        self.nlanes = 0
        self.maxlanes = nlanes
        for e in ("pe", "act", "dve", "pool"):
            self.eng_semkey[e] = self._newsem(f"s_{e}")
        for i in range(nlanes):
            self._newsem(f"lane{i + 1}")

    def _newsem(self, key):
        s = self.es.enter_context(self.nc.semaphore(key))
        self.sems[key] = s
        self.cnt[key] = 0
        return key

    def lane(self):
        self.nlanes += 1
        assert self.nlanes <= self.maxlanes, "out of DMA lanes"
        return f"lane{self.nlanes}"

    def _deps(self, eng, reads, writes):
        need = {}
        gen = self.gen
        for b in reads:
            w = b.writer
            if w is not None and w[2] == gen:
                if need.get(w[0], 0) < w[1]:
                    need[w[0]] = w[1]
        for b in writes:
            w = b.writer
            if w is not None and w[2] == gen:
                if need.get(w[0], 0) < w[1]:
                    need[w[0]] = w[1]
            for (k, v, g) in b.readers:
                if g == gen and need.get(k, 0) < v:
                    need[k] = v
        waits = []
        seen = self.seen[eng]
        pek = self.eng_semkey["pe"]
        for k, v in need.items():
            if eng == "pe" and k == pek:
                continue
            if seen.get(k, 0) >= v:
                continue
            seen[k] = v
            waits.append((k, v))
        return waits

    @staticmethod
    def _commit(ref, reads, writes):
        for b in reads:
            if b.readers and b.readers[0][2] != ref[2]:
                b.readers = []
            b.readers.append(ref)
        for b in writes:
            b.writer = ref
            b.readers = []

    def _emit_waits(self, eng, waits):
        e = self.eng[eng]
        for k, v in waits:
            e.wait_ge(self.sems[k], v)
        return e

    def op(self, eng, fn, reads=(), writes=()):
        waits = self._deps(eng, reads, writes)
        k = self.eng_semkey[eng]
        self.cnt[k] += 1
        self._commit((k, self.cnt[k], self.gen), reads, writes)
        e = self._emit_waits(eng, waits)
        fn(e).then_inc(self.sems[k], 1)

    def dma(self, queue, out, in_, reads=(), writes=(), lane=None, cont=False):
        assert lane is not None
        waits = self._deps(queue, reads, writes)
        if not cont:
            v = self.cnt[lane]
            if v > 0 and self.seen[queue].get(lane, 0) < v:
                self.seen[queue][lane] = v
                waits.append((lane, v))
        self.cnt[lane] += 16
        self._commit((lane, self.cnt[lane], self.gen), reads, writes)
        e = self._emit_waits(queue, waits)
        e.dma_start(out=out, in_=in_).then_inc(self.sems[lane], 16)

    def barrier(self, engines=ENGS):
        for eng in engines:
            waits = []
            seen = self.seen[eng]
            for k, v in self.cnt.items():
                if v > 0 and seen.get(k, 0) < v:
                    seen[k] = v
                    waits.append((k, v))
            self._emit_waits(eng, waits)

    def hard_reset(self):
        self.barrier()
        self.nc.all_engine_barrier()
        for k, v in self.cnt.items():
            if v > 0:
                self.nc.gpsimd.sem_clear(self.sems[k])
        self.nc.all_engine_barrier()
        for k in self.cnt:
            self.cnt[k] = 0
        self.seen = {e: {} for e in ENGS}
        self.gen += 1
        self.nlanes = 0


class Prog:
    def __init__(self, nl=NL, stop_after=None, small=False, dbg=False):
        self.nl = nl
        self.stop_after = stop_after
        self.small = small
        self.sub = None
        if stop_after in ("A0", "A1"):
            self.sub = stop_after
            self.stop_after = "A"
        wm = (lambda n: min(n, 4)) if small else (lambda n: n)
        self.wi = (lambda nb: nb % 4) if small else (lambda nb: nb)
        self.nc = bass.Bass("TRN2", target_bir_lowering=False)
        nc = self.nc

        def din(name, shape, dt=F32):
            return nc.dram_tensor(name, shape, dt, kind="ExternalInput").ap()

        def dout(name, shape, dt=F32):
            return nc.dram_tensor(name, shape, dt, kind="ExternalOutput").ap()

        def dscr(name, shape, dt):
            if dbg and name in ("XT", "QT", "KF0", "KF1", "GF0", "GF1", "VT", "SG", "U", "SGC", "MH", "MC", "OF", "OHG", "YC", "OCV"):
                return nc.dram_tensor(name, shape, dt, kind="ExternalOutput").ap()
            return nc.dram_tensor(name, shape, dt).ap()
        self.xT = din("xT", [D, T])
        self.state0 = din("state0", [NL * 2 * H, 128, 128])
        self.cT = din("cT", [128, KC * 2])
        self.modw = din("modw", [NL, wm(96), 128, KC * 128])
        self.modb = din("modb", [128, NL * 96])
        self.normg = din("normg", [128, NL * KC])
        self.nbw = wm(192)
        self.nbp = wm(32)
        self.win = din("win", [NL * wm(192), 128, KC * 128])
        self.lbl = din("lbl", [128, NL * 32])
        self.hgng = din("hgng", [128, NL])
        self.cvw = din("cvw", [128, NL * 16 * CW])
        self.cvb = din("cvb", [128, NL * 16])
        self.lng = din("lng", [128, NL * 16])
        self.lnb = din("lnb", [128, NL * 16])
        self.whp = din("whp", [NL * wm(32), 128, 16 * 128])
        self.wcp = din("wcp", [NL * wm(32), 128, 16 * 128])
        self.wout = din("wout", [NL * wm(32), 128, KC * 128])
        self.fing = din("fing", [128, KC])
        self.ident = din("ident", [128, 128])
        self.cmask = din("cmask", [32, 2 * 512])
        self.yT = dout("yT", [D, T])
        self.stout = dout("stout", [NPR * NL * 2 * H, 128, 128])
        self.XT = dscr("XT", [D, T], F32)
        self.QT = dscr("QT", [DH, T], BF16)
        self.KF = [dscr("KF0", [DH, T], BF16), dscr("KF1", [DH, T], BF16)]
        self.GF = [dscr("GF0", [DH, T], F32), dscr("GF1", [DH, T], F32)]
        self.VT = dscr("VT", [DH, T], BF16)
        self.SG = dscr("SG", [DH, T], BF16)
        self.U = dscr("U", [DH, T], F32)
        self.SGC = dscr("SGC", [DH, T], BF16)
        self.MH = dscr("MH", [D, T], BF16)
        self.MC = dscr("MC", [D, T], BF16)
        self.OF = dscr("OF", [DH, T], F32)
        self.OHG = dscr("OHG", [DH, T], BF16)
        self.YC = dscr("YC", [DH, T], F32)
        self.OCV = dscr("OCV", [DH, T], BF16)
        self.nbh = self.nbw // 2
        self.cwin2 = [dscr("cwin0", [self.nbh, 128, KC * 128], F32), dscr("cwin1", [self.nbh, 128, KC * 128], F32)]
        self.cstate = dscr("cstate", [2 * H, 128, 128], F32)
        self.cst = dscr("cst", [NPR * 2 * H, 128, 128], F32)
        self.cwhp = dscr("cwhp", [self.nbp, 128, 16 * 128], F32)
        self.cwcp = dscr("cwcp", [self.nbp, 128, 16 * 128], F32)
        self.cwout = dscr("cwout", [self.nbp, 128, KC * 128], F32)
        self.seqs = [(0, TS, "sample", 0), (TS, TPR, "prompt", 0), (TS + TPR, TPR, "prompt", 1)]

    def tile(self, ap, lane=False, name="t"):
        return Tile(ap, Buf(name, self.cx.lane() if lane else None))

    def tf(self, n, lane=False, name="t"):
        return self.tile(self.A.f32(n), lane, name)

    def tb(self, n, lane=False, name="t"):
        return self.tile(self.A.bf16(n), lane, name)

    def phase(self):
        self.cx.hard_reset()
        self.A.reset()

    def build(self):
        nc = self.nc
        with ExitStack() as es:
            self.cx = cx = Ctx(nc, es)
            arena = es.enter_context(nc.sbuf_tensor("arena", [128, 44600], F32))
            parena = es.enter_context(nc.sbuf_tensor("parena", [128, 7800], F32))
            self.A = Arena(arena[:])
            self.PA = Arena(parena[:])
            self.P = []
            for i in range(8):
                p = es.enter_context(nc.psum_tensor(f"ps{i}", [128, 512], F32))
                self.P.append(Tile(p[:], Buf(f"ps{i}")))
            self.phase_init()
            stop = self.stop_after
            if self.nl > 0 and stop != "init":
                cx.hard_reset()
                with nc.Fori(0, self.nl) as li:
                    self.li = li
                    self.is_row = (li % 2) == 0
                    self.layer_params()
                    if stop != "P":
                        self.phase_A()
                    if stop not in ("A", "P"):
                        self.phase_S()
                        if stop != "S":
                            self.phase_V()
                            if stop != "V":
                                self.phase_C()
                    if stop not in ("A", "P"):
                        self.layer_end()
                    cx.hard_reset()
            if stop is None:
                self.phase_final()
            cx.barrier()
        return nc

    def layer_params(self):
        cx, li = self.cx, self.li
        for (cur, src, n) in self.cur_list:
            cx.op("dve", lambda e, cur=cur, src=src, n=n: e.tensor_copy(out=cur.ap, in_=src.ap[:, bass.ts(li, n)]), reads=[src.b], writes=[cur.b])
        nc = self.nc
        ln = cx.lane()
        jobs = [(self.cwin2[0], self.win, self.nbw, 0), (self.cwin2[1], self.win, self.nbw, self.nbh),
                (self.cwhp, self.whp, self.nbp, 0), (self.cwcp, self.wcp, self.nbp, 0), (self.cwout, self.wout, self.nbp, 0),
                (self.cstate, self.state0, 2 * H, 0)]
        ndma = 0
        for arm in range(self.nl):
            ndma = 0
            with nc.sync.If(li == arm):
                for (dst, src, nblk, off) in jobs:
                    n = dst.shape[0]
                    step = min(n, 8)
                    for a in range(0, n, step):
                        s0 = arm * nblk + off + a
                        nc.sync.dma_start(out=dst[a:a + step], in_=src[s0:s0 + step]).then_inc(cx.sems[ln], 16)
                        ndma += 1
        cx.cnt[ln] += 16 * ndma

    def layer_end(self):
        cx, nc, li = self.cx, self.nc, self.li
        cx.barrier(("sp",))
        ln = cx.lane()
        for arm in range(self.nl):
            with nc.sync.If(li == arm):
                for p in range(NPR):
                    d0 = (p * NL + arm) * 2 * H
                    nc.sync.dma_start(out=self.stout[d0:d0 + 2 * H], in_=self.cst[p * 2 * H:(p + 1) * 2 * H]).then_inc(cx.sems[ln], 16)
        cx.cnt[ln] += 16 * NPR

    def phase_init(self):
        cx, PA, P = self.cx, self.PA, self.P
        ld = cx.lane()

        def pload(n, src, name):
            t = Tile(PA.f32(n), Buf(name))
            cx.dma("sp", t.ap, src, writes=[t.b], lane=ld, cont=True)
            return t
        self.t_modb = pload(NL * 96, self.modb, "modb")
        self.t_normg = pload(NL * KC, self.normg, "normg")
        self.t_lbl = pload(NL * 32, self.lbl, "lbl")
        self.t_hgng = pload(NL, self.hgng, "hgng")
        self.t_cvw = pload(NL * 16 * CW, self.cvw, "cvw")
        self.t_cvb = pload(NL * 16, self.cvb, "cvb")
        self.t_lng = pload(NL * 16, self.lng, "lng")
        self.t_lnb = pload(NL * 16, self.lnb, "lnb")
        self.t_fing = pload(KC, self.fing, "fing")
        t_identf = pload(128, self.ident, "identf")
        t_cT = pload(KC * 2, self.cT, "cT")
        self.t_cmask = Tile(PA.f32(1024), Buf("cmask"))
        cx.dma("sp", self.t_cmask.ap[0:32, :], self.cmask, writes=[self.t_cmask.b], lane=ld, cont=True)
        for t_ in (self.t_modb, self.t_normg, self.t_lbl, self.t_hgng, self.t_cvw, self.t_cvb, self.t_lng, self.t_lnb, self.t_fing,
                   t_identf, t_cT, self.t_cmask):
            t_.b.writer = (ld, cx.cnt[ld], cx.gen)
        self.t_idb = Tile(PA.bf16(128), Buf("idb"))
        self.t_ones = Tile(PA.f32(128), Buf("ones"))
        self.t_chm = Tile(PA.f32(512), Buf("chm"))
        self.t_mod = Tile(PA.f32(NL * 96 * 2), Buf("mod"))
        self.t_gs = Tile(PA.f32(NL * KC * 2), Buf("gs"))
        self.t_lb = Tile(PA.f32(NL * 32), Buf("lb"))
        self.t_oml = Tile(PA.f32(NL * 32), Buf("oml"))
        self.t_noml = Tile(PA.f32(NL * 32), Buf("noml"))
        t_sc = Tile(PA.f32(KC * 2), Buf("sc"))
        idb, ones, chm = self.t_idb, self.t_ones, self.t_chm
        cx.op("dve", lambda e: e.tensor_copy(out=idb.ap, in_=t_identf.ap), reads=[t_identf.b], writes=[idb.b])
        cx.op("dve", lambda e: e.memset(ones.ap, 1.0), writes=[ones.b])

        cx.op("pool", lambda e: e.memset(chm.ap, 1.0), writes=[chm.b])
        cx.op("pool", lambda e: e.memset(chm.ap.rearrange("p (c t) -> p c t", t=32)[:, :, 0:1], 0.0), writes=[chm.b])
        cx.op("act", lambda e: e.activation(out=t_sc.ap, in_=t_cT.ap, func=AF.Silu), reads=[t_cT.b], writes=[t_sc.b])
        A = self.A
        lbl3 = self.t_lbl.ap.rearrange("p (l n) -> p l n", n=32)
        mx = self.tf(32, name="mx")
        ee = self.tf(NL * 32, name="ee")
        sm = self.tf(32, name="sm")
        ee3 = ee.ap.rearrange("p (l n) -> p l n", n=32)

        cx.op("dve", lambda e: e.tensor_max(out=mx.ap, in0=lbl3[:, 0, :], in1=lbl3[:, 1, :]), reads=[self.t_lbl.b], writes=[mx.b])
        for l in range(2, NL):
            cx.op("dve", lambda e, l=l: e.tensor_max(out=mx.ap, in0=mx.ap, in1=lbl3[:, l, :]), reads=[self.t_lbl.b, mx.b], writes=[mx.b])
        cx.op("dve", lambda e: e.tensor_tensor(out=ee3, in0=lbl3, in1=mx.ap.rearrange("p (o n) -> p o n", o=1).to_broadcast([128, NL, 32]),
                                                op=ALU.subtract), reads=[self.t_lbl.b, mx.b], writes=[ee.b])
        cx.op("act", lambda e: e.activation(out=ee.ap, in_=ee.ap, func=AF.Exp), reads=[ee.b], writes=[ee.b])

        cx.op("dve", lambda e: e.tensor_add(out=sm.ap, in0=ee3[:, 0, :], in1=ee3[:, 1, :]), reads=[ee.b], writes=[sm.b])
        for l in range(2, NL):
            cx.op("dve", lambda e, l=l: e.tensor_add(out=sm.ap, in0=sm.ap, in1=ee3[:, l, :]), reads=[ee.b, sm.b], writes=[sm.b])
        cx.op("dve", lambda e: e.reciprocal(out=sm.ap, in_=sm.ap), reads=[sm.b], writes=[sm.b])
        cx.op("dve", lambda e: e.tensor_tensor(out=ee3, in0=ee3, in1=sm.ap.rearrange("p (o n) -> p o n", o=1).to_broadcast([128, NL, 32]),
                                                op=ALU.mult), reads=[ee.b, sm.b], writes=[ee.b])
        lb3 = self.t_lb.ap.rearrange("p (l n) -> p l n", n=32)
        cx.op("dve", lambda e: e.memset(lb3[:, 0, :], 0.0), writes=[self.t_lb.b])
        cx.op("dve", lambda e: e.tensor_copy(out=lb3[:, 1, :], in_=ee3[:, 1, :]), reads=[ee.b], writes=[self.t_lb.b])
        for l in range(2, NL):
            cx.op("dve", lambda e, l=l: e.tensor_add(out=lb3[:, l, :], in0=lb3[:, l - 1, :], in1=ee3[:, l, :]),
                  reads=[ee.b, self.t_lb.b], writes=[self.t_lb.b])
        cx.op("dve", lambda e: e.tensor_scalar(out=self.t_oml.ap, in0=self.t_lb.ap, scalar1=-1.0, scalar2=1.0, op0=ALU.mult, op1=ALU.add),
              reads=[self.t_lb.b], writes=[self.t_oml.b])
        cx.op("dve", lambda e: e.tensor_scalar(out=self.t_noml.ap, in0=self.t_lb.ap, scalar1=-1.0, scalar2=None, op0=ALU.add),
              reads=[self.t_lb.b], writes=[self.t_noml.b])
        wr = Ring([self.tf(KC * 128, lane=True, name=f"mw{i}") for i in range(3)])
        sc3 = t_sc.ap.rearrange("p (k r) -> p k r", r=2)
        mod4 = self.t_mod.ap.rearrange("p (l n r) -> p l n r", n=96, r=2)
        modb3 = self.t_modb.ap.rearrange("p (l n) -> p l n", n=96)
        qi = 0
        for l in range(self.nl):
            pm = P[l % 2]
            pm3 = pm.ap[:, 0:192].rearrange("p (n r) -> p n r", r=2)
            for nb in range(96):
                w = wr.next()
                cx.dma("sp", w.ap, self.modw[l, self.wi(nb)], writes=[w.b], lane=w.b.lane)
                qi += 1

                def f_mm(e, w=w, nb=nb, pm3=pm3):
                    for kc in range(KC):
                        last = e.matmul(pm3[:, nb, :], lhsT=w.ap[:, kc * 128:(kc + 1) * 128], rhs=sc3[:, kc, :],
                                        start=(kc == 0), stop=(kc == KC - 1))
                    return last
                cx.op("pe", f_mm, reads=[w.b, t_sc.b], writes=[pm.b])
            cx.op("dve", lambda e, l=l, pm3=pm3: e.tensor_tensor(
                out=mod4[:, l], in0=pm3, in1=modb3[:, l, :].rearrange("p (n o) -> p n o", o=1).to_broadcast([128, 96, 2]), op=ALU.add),
                reads=[pm.b, self.t_modb.b], writes=[self.t_mod.b])
        gs4 = self.t_gs.ap.rearrange("p (l k r) -> p l k r", k=KC, r=2)
        ng3 = self.t_normg.ap.rearrange("p (l k) -> p l k", k=KC)
        for l in range(self.nl):
            cx.op("dve", lambda e, l=l: e.tensor_scalar(out=gs4[:, l], in0=mod4[:, l, 32:64, :], scalar1=1.0, scalar2=None, op0=ALU.add),
                  reads=[self.t_mod.b], writes=[self.t_gs.b])
            cx.op("dve", lambda e, l=l: e.tensor_tensor(out=gs4[:, l], in0=gs4[:, l],
                                                        in1=ng3[:, l, :].rearrange("p (k o) -> p k o", o=1).to_broadcast([128, KC, 2]), op=ALU.mult),
                  reads=[self.t_gs.b, self.t_normg.b], writes=[self.t_gs.b])
        def cur(n, name):
            return Tile(PA.f32(n), Buf(name))
        self.c_mod, self.c_gs = cur(192, "c_mod"), cur(64, "c_gs")
        self.c_lb, self.c_oml, self.c_noml = cur(32, "c_lb"), cur(32, "c_oml"), cur(32, "c_noml")
        self.c_hgng = cur(1, "c_hgng")
        self.c_cvw, self.c_cvb, self.c_lng, self.c_lnb = cur(16 * CW, "c_cvw"), cur(16, "c_cvb"), cur(16, "c_lng"), cur(16, "c_lnb")
        self.cur_list = [(self.c_mod, self.t_mod, 192), (self.c_gs, self.t_gs, 64), (self.c_lb, self.t_lb, 32), (self.c_oml, self.t_oml, 32),
                         (self.c_noml, self.t_noml, 32), (self.c_hgng, self.t_hgng, 1), (self.c_cvw, self.t_cvw, 16 * CW),
                         (self.c_cvb, self.t_cvb, 16), (self.c_lng, self.t_lng, 16), (self.c_lnb, self.t_lnb, 16)]
        self.cmod3 = self.c_mod.ap.rearrange("p (n r) -> p n r", r=2)
        self.cgs3 = self.c_gs.ap.rearrange("p (k r) -> p k r", r=2)
        cl = cx.lane()
        for kc in range(KC):
            cx.dma("sp", self.XT[kc * 128:(kc + 1) * 128, :], self.xT[kc * 128:(kc + 1) * 128, :], lane=cl, cont=True)

    def norm_block(self, src, tok0, pst, xring, sqring, rstd):
        cx = self.cx
        srcv = src.rearrange("(kc p) t -> p kc t", p=128)
        ones = self.t_ones
        for pc in range(NPC):
            xs = xring.next()
            cx.dma("sp", xs.ap.rearrange("p (k t) -> p k t", t=512), srcv[:, pc * PCS:(pc + 1) * PCS, tok0:tok0 + 512],
                   reads=[self.xbuf], writes=[xs.b], lane=xs.b.lane)
            sq = sqring.next()
            cx.op("act", lambda e, xs=xs, sq=sq: e.activation(out=sq.ap, in_=xs.ap, func=AF.Square), reads=[xs.b], writes=[sq.b])

            def f_mm(e, sq=sq, pc=pc):
                for j in range(PCS):
                    last = e.matmul(pst.ap, lhsT=ones.ap, rhs=sq.ap[:, j * 512:(j + 1) * 512],
                                    start=(pc == 0 and j == 0), stop=(pc == NPC - 1 and j == PCS - 1))
                return last
            cx.op("pe", f_mm, reads=[sq.b, ones.b], writes=[pst.b])
        cx.op("act", lambda e: e.activation(out=rstd.ap, in_=pst.ap, func=AF.Sqrt, scale=1.0 / D, bias=EPS), reads=[pst.b], writes=[rstd.b])
        cx.op("dve", lambda e: e.reciprocal(out=rstd.ap, in_=rstd.ap), reads=[rstd.b], writes=[rstd.b])

    def phase_A(self):
        cx, P = self.cx, self.P
        src = self.XT
        srcv = src.rearrange("(kc p) t -> p kc t", p=128)
        lb, oml, noml = self.c_lb, self.c_oml, self.c_noml
        for g in range(3):
            self.phase()
            self.xbuf = Buf("xsrc")
            hT = self.A.bf16(KC * 1536)
            hT3 = hT.rearrange("p (k t) -> p k t", t=1536)
            hb = [Buf(f"hT{i}") for i in range(3)]
            xring = Ring([self.tf(PCS * 512, lane=True, name=f"x{i}") for i in range(2)])
            sqring = Ring([self.tf(PCS * 512, name=f"sq{i}") for i in range(2)])
            tmpring = Ring([self.tf(PCS * 512, name=f"tmp{i}") for i in range(2)])
            rstd = self.tf(512, name="rstd")
            wring = Ring([self.tb(KC * 128, name=f"w{i}") for i in range(3)])
            stg = Ring([self.tf(2048, lane=True, name=f"stg{i}") for i in range(2)])
            stage = Ring([self.tf(512, lane=True, name=f"st{i}") for i in range(4)])
            tring = Ring([self.tf(512, name=f"tt{i}") for i in range(2)])
            pring = Ring(P[0:6])
            pst = P[7]
            for bi in range(3):
                blk = g * 3 + bi
                r = 0 if blk < 8 else 1
                tok0 = blk * 512
                self.norm_block(src, tok0, pst, xring, sqring, rstd)
                for pc in range(NPC):
                    xs = xring.next()
                    cx.dma("sp", xs.ap.rearrange("p (k t) -> p k t", t=512), srcv[:, pc * PCS:(pc + 1) * PCS, tok0:tok0 + 512],
                           reads=[self.xbuf], writes=[xs.b], lane=xs.b.lane)
                    tmp = tmpring.next()
                    cx.op("dve", lambda e, xs=xs, tmp=tmp: e.tensor_tensor(
                        out=tmp.ap.rearrange("p (k t) -> p k t", t=512), in0=xs.ap.rearrange("p (k t) -> p k t", t=512),
                        in1=rstd.ap.rearrange("p (o t) -> p o t", o=1).to_broadcast([128, PCS, 512]), op=ALU.mult),
                        reads=[xs.b, rstd.b], writes=[tmp.b])

                    def f_h(e, tmp=tmp, pc=pc, bi=bi, r=r):
                        for j in range(PCS):
                            kc = pc * PCS + j
                            last = e.activation(out=hT3[:, kc, bi * 512:(bi + 1) * 512], in_=tmp.ap[:, j * 512:(j + 1) * 512],
                                                func=AF.Identity, scale=self.cgs3[:, kc, r:r + 1], bias=self.cmod3[:, kc, r:r + 1])
                        return last
                    cx.op("act", f_h, reads=[tmp.b, self.c_gs.b, self.c_mod.b], writes=[hb[bi]])

            if self.sub == "A0":
                return

            def mm_group(w, pb, tbi):
                def f(e):
                    for kc in range(KC):
                        last = e.matmul(pb.ap, lhsT=w.ap[:, kc * 128:(kc + 1) * 128], rhs=hT3[:, kc, tbi * 512:(tbi + 1) * 512],
                                        start=(kc == 0), stop=(kc == KC - 1))
                    return last
                cx.op("pe", f, reads=[w.b, hb[tbi]], writes=[pb.b])

            def load_w(nb):
                w = wring.next()
                srcw = self.cwin2[self.wi(nb) // self.nbh][self.wi(nb) % self.nbh]
                for hh in range(2):
                    sg_ = stg.next()
                    cx.dma("sp", sg_.ap, srcw[:, hh * 2048:(hh + 1) * 2048], writes=[sg_.b], lane=sg_.b.lane)
                    cx.op("pool", lambda e, w=w, sg_=sg_, hh=hh: e.tensor_copy(out=w.ap[:, hh * 2048:(hh + 1) * 2048], in_=sg_.ap),
                          reads=[sg_.b], writes=[w.b])
                return w

            def store(dst, row0, tok0, st, as_bf16):
                src_ap = st.ap.bitcast(BF16)[:, 0:512] if as_bf16 else st.ap
                cx.dma("sp", dst[row0:row0 + 128, tok0:tok0 + 512], src_ap, reads=[st.b], writes=[], lane=st.b.lane)

            def simple(nb0, nbn, func, dst):
                for nb in range(nb0, nb0 + nbn):
                    w = load_w(nb)
                    for tbi in range(3):
                        pb = pring.next()
                        mm_group(w, pb, tbi)
                        st = stage.next()
                        cx.op("act", lambda e, pb=pb, st=st: e.activation(out=st.ap.bitcast(BF16)[:, 0:512], in_=pb.ap, func=func),
                              reads=[pb.b], writes=[st.b])
                        store(dst, (nb - nb0) * 128, (g * 3 + tbi) * 512, st, True)

            def gates(nb0, d):
                for hh in range(H):
                    w = load_w(nb0 + hh)
                    col = d * 16 + hh
                    for tbi in range(3):
                        pb = pring.next()
                        mm_group(w, pb, tbi)
                        tt = tring.next()
                        cx.op("act", lambda e, pb=pb, tt=tt: e.activation(out=tt.ap, in_=pb.ap, func=AF.Sigmoid), reads=[pb.b], writes=[tt.b])
                        st = stage.next()
                        cx.op("act", lambda e, tt=tt, st=st, col=col: e.activation(
                            out=st.ap, in_=tt.ap, func=AF.Ln, scale=oml.ap[:, col:col + 1], bias=lb.ap[:, col:col + 1]),
                            reads=[tt.b, oml.b, lb.b], writes=[st.b])
                        store(self.GF[d], hh * 128, (g * 3 + tbi) * 512, st, False)
                        st2 = stage.next()
                        cx.op("dve", lambda e, tt=tt, st2=st2, col=col: e.tensor_scalar(
                            out=st2.ap.bitcast(BF16)[:, 0:512], in0=tt.ap, scalar1=noml.ap[:, col:col + 1], scalar2=oml.ap[:, col:col + 1],
                            op0=ALU.mult, op1=ALU.add), reads=[tt.b, oml.b, noml.b], writes=[st2.b])
                        store(self.KF[d], hh * 128, (g * 3 + tbi) * 512, st2, True)

            def glu():
                for j in range(16):
                    wa = load_w(80 + j)
                    wb = load_w(96 + j)
                    for tbi in range(3):
                        pa = pring.next()
                        mm_group(wa, pa, tbi)
                        pb = pring.next()
                        mm_group(wb, pb, tbi)
                        tt = tring.next()
                        cx.op("act", lambda e, pb=pb, tt=tt: e.activation(out=tt.ap, in_=pb.ap, func=AF.Sigmoid), reads=[pb.b], writes=[tt.b])
                        st = stage.next()
                        cx.op("dve", lambda e, pa=pa, tt=tt, st=st: e.tensor_tensor(out=st.ap, in0=pa.ap, in1=tt.ap, op=ALU.mult),
                              reads=[pa.b, tt.b], writes=[st.b])
                        store(self.U, j * 128, (g * 3 + tbi) * 512, st, False)
            simple(0, 16, AF.Silu, self.QT)
            if self.sub == "A1":
                return
            simple(64, 16, AF.Silu, self.SG)
            simple(112, 16, AF.Silu, self.SGC)
            simple(48, 16, AF.Identity, self.VT)
            simple(128, 32, AF.Sigmoid, self.MH)
            simple(160, 32, AF.Sigmoid, self.MC)
            glu()
            gates(16, 0)
            gates(32, 1)

    def phase_S(self):
        cx, P = self.cx, self.P
        self.phase()
        NCH = 3
        res = []
        for k in range(NCH):
            R = dict(
                g=self.tf(512, lane=True), b=self.tf(512), b2=self.tf(512), eb=self.tf(512), enb=self.tf(512),
                of=self.tf(512, lane=True), oacc=self.tf(512), tmp=self.tf(512, lane=True),
                q=self.tb(512, lane=True), k=self.tb(512, lane=True), v=self.tb(512, lane=True), sg=self.tb(512, lane=True),
                qb=self.tb(512), kb=self.tb(512), kend=self.tb(512), ost=self.tb(512, lane=True),
                kendc=self.tb(16 * 128), vc=self.tb(16 * 128), sc=self.tb(512),
                S32=self.tf(128, lane=True), Sbf=[self.tb(128), self.tb(128)], po=P[k])
            res.append(R)
        self.pu_ring = Ring([Tile(P[3].ap[:, 0:128], P[3].b), Tile(P[5].ap[:, 0:128], P[5].b)])
        self.pt_ring = Ring([Tile(P[6].ap.bitcast(BF16), Buf("pt0")), Tile(P[7].ap.bitcast(BF16), Buf("pt1"))])
        self.psc = P[4]
        for d in range(2):
            if d == 1:
                cx.barrier()
            chains = []
            for (tok0, L, kind, pidx) in self.seqs:
                for h in range(H):
                    chains.append((tok0, L, kind, pidx, h))
            for i in range(0, len(chains), NCH):
                gens = [self.scan_chain(d, chains[i + k], res[k]) for k in range(NCH) if i + k < len(chains)]
                while gens:
                    for gn in list(gens):
                        try:
                            next(gn)
                        except StopIteration:
                            gens.remove(gn)

    def scan_chain(self, d, chain, R):
        cx = self.cx
        tok0, L, kind, pidx, h = chain
        LB = min(512, L)
        nblk = L // LB
        nch = LB // 32
        r0 = h * 128
        S32, Sbf, po = R["S32"], R["Sbf"], R["po"]
        idb = self.t_idb
        cm = self.t_cmask.ap[0:32, d * 512:d * 512 + nch * 32].rearrange("p (c t) -> p c t", t=32)
        if kind == "sample":
            cx.dma("sp", S32.ap, self.cstate[d * H + h], writes=[S32.b], lane=S32.b.lane)
        else:
            cx.op("dve", lambda e: e.memset(S32.ap, 0.0), writes=[S32.b])
        sbi = 0
        cx.op("act", lambda e, o=Sbf[0]: e.activation(out=o.ap, in_=S32.ap, func=AF.Identity), reads=[S32.b], writes=[Sbf[0].b])
        yield
        blocks = list(range(nblk)) if d == 0 else list(reversed(range(nblk)))
        epos = 31 if d == 0 else 0
        for bi in blocks:
            t0 = tok0 + bi * LB
            g, b, b2, eb, enb = R["g"], R["b"], R["b2"], R["eb"], R["enb"]
            q, k, v, qb, kb, kend = R["q"], R["k"], R["v"], R["qb"], R["kb"], R["kend"]
            kendc, vc, sc = R["kendc"], R["vc"], R["sc"]
            cx.dma("sp", g.ap[:, 0:LB], self.GF[d][r0:r0 + 128, t0:t0 + LB], writes=[g.b], lane=g.b.lane)
            cx.dma("sp", q.ap[:, 0:LB], self.QT[r0:r0 + 128, t0:t0 + LB], writes=[q.b], lane=q.b.lane)
            cx.dma("sp", k.ap[:, 0:LB], self.KF[d][r0:r0 + 128, t0:t0 + LB], writes=[k.b], lane=k.b.lane)
            cx.dma("sp", v.ap[:, 0:LB], self.VT[r0:r0 + 128, t0:t0 + LB], writes=[v.b], lane=v.b.lane)
            cx.op("dve", lambda e: e.tensor_tensor_scan(out=b.ap[:, 0:LB], data0=self.t_chm.ap[:, 0:LB], data1=g.ap[:, 0:LB], initial=0.0,
                                                        op0=ALU.mult, op1=ALU.add), reads=[g.b, self.t_chm.b], writes=[b.b])
            bb = b
            if d == 1:
                cx.op("dve", lambda e: e.tensor_tensor(out=b2.ap[:, 0:LB], in0=g.ap[:, 0:LB], in1=b.ap[:, 0:LB], op=ALU.subtract),
                      reads=[g.b, b.b], writes=[b2.b])
                cx.op("dve", lambda e: e.tensor_tensor(out=b2.ap[:, 0:LB].rearrange("p (c t) -> p c t", t=32),
                                                        in0=b2.ap[:, 0:LB].rearrange("p (c t) -> p c t", t=32),
                                                        in1=b.ap[:, 0:LB].rearrange("p (c t) -> p c t", t=32)[:, :, 31:32].to_broadcast([128, nch, 32]),
                                                        op=ALU.add), reads=[b2.b, b.b], writes=[b2.b])
                bb = b2
            cx.op("act", lambda e, bb=bb: e.activation(out=eb.ap[:, 0:LB], in_=bb.ap[:, 0:LB], func=AF.Exp), reads=[bb.b], writes=[eb.b])
            cx.op("act", lambda e, bb=bb: e.activation(out=enb.ap[:, 0:LB], in_=bb.ap[:, 0:LB], func=AF.Exp, scale=-1.0), reads=[bb.b], writes=[enb.b])
            cx.op("dve", lambda e: e.tensor_tensor(out=qb.ap[:, 0:LB], in0=q.ap[:, 0:LB], in1=eb.ap[:, 0:LB], op=ALU.mult),
                  reads=[q.b, eb.b], writes=[qb.b])
            cx.op("dve", lambda e: e.tensor_tensor(out=kb.ap[:, 0:LB], in0=k.ap[:, 0:LB], in1=enb.ap[:, 0:LB], op=ALU.mult),
                  reads=[k.b, enb.b], writes=[kb.b])
            eb3 = eb.ap[:, 0:LB].rearrange("p (c t) -> p c t", t=32)
            cx.op("dve", lambda e: e.tensor_tensor(out=kend.ap[:, 0:LB].rearrange("p (c t) -> p c t", t=32),
                                                    in0=kb.ap[:, 0:LB].rearrange("p (c t) -> p c t", t=32),
                                                    in1=eb3[:, :, epos:epos + 1].to_broadcast([128, nch, 32]), op=ALU.mult),
                  reads=[kb.b, eb.b], writes=[kend.b])
            for (srcT, dstc) in ((kend, kendc), (v, vc)):
                for c0 in range(0, nch, 8):
                    pt = self.pt_ring.next()
                    pt3 = pt.ap[0:32, :].rearrange("p (c k) -> p c k", k=128)

                    def f_tr(e, srcT=srcT, c0=c0, pt3=pt3):
                        for j in range(8):
                            c = c0 + j
                            last = e.transpose(out=pt3[:, j, :], in_=srcT.ap[:, c * 32:(c + 1) * 32], identity=idb.ap)
                        return last
                    cx.op("pe", f_tr, reads=[srcT.b, idb.b], writes=[pt.b])
                    cx.op("act", lambda e, pt=pt, dstc=dstc, c0=c0: e.activation(
                        out=dstc.ap[0:32, c0 * 128:(c0 + 8) * 128], in_=pt.ap[0:32, :], func=AF.Identity), reads=[pt.b], writes=[dstc.b])
            psc = self.psc
            psc3 = psc.ap[0:32, 0:nch * 32].rearrange("p (c t) -> p c t", t=32)

            def f_sc(e):
                for c in range(nch):
                    last = e.matmul(psc3[:, c, :], lhsT=kb.ap[:, c * 32:(c + 1) * 32], rhs=qb.ap[:, c * 32:(c + 1) * 32], start=True, stop=True)
                return last
            cx.op("pe", f_sc, reads=[kb.b, qb.b], writes=[psc.b])
            sc3 = sc.ap[0:32, 0:nch * 32].rearrange("p (c t) -> p c t", t=32)
            cx.op("dve", lambda e: e.tensor_tensor(out=sc3, in0=psc3, in1=cm, op=ALU.mult), reads=[psc.b, self.t_cmask.b], writes=[sc.b])
            yield
            chunks = list(range(nch)) if d == 0 else list(reversed(range(nch)))
            for c in chunks:
                cur = Sbf[sbi % 2]
                nxt = Sbf[(sbi + 1) % 2]
                sbi += 1

                def f_o(e, c=c, cur=cur):
                    e.matmul(po.ap[:, c * 32:(c + 1) * 32], lhsT=vc.ap[0:32, c * 128:(c + 1) * 128], rhs=sc3[:, c, :], start=True, stop=False)
                    return e.matmul(po.ap[:, c * 32:(c + 1) * 32], lhsT=cur.ap, rhs=qb.ap[:, c * 32:(c + 1) * 32], start=False, stop=True)
                cx.op("pe", f_o, reads=[vc.b, sc.b, cur.b, qb.b], writes=[po.b])
                pu = self.pu_ring.next()
                cx.op("pe", lambda e, c=c, pu=pu: e.matmul(pu.ap, lhsT=kendc.ap[0:32, c * 128:(c + 1) * 128], rhs=vc.ap[0:32, c * 128:(c + 1) * 128],
                                                           start=True, stop=True), reads=[kendc.b, vc.b], writes=[pu.b])
                pos = c * 32 + epos
                cx.op("dve", lambda e, pu=pu, pos=pos: e.scalar_tensor_tensor(out=S32.ap, in0=S32.ap, scalar=eb.ap[:, pos:pos + 1], in1=pu.ap,
                                                                              op0=ALU.mult, op1=ALU.add),
                      reads=[S32.b, eb.b, pu.b], writes=[S32.b])
                cx.op("act", lambda e, nxt=nxt: e.activation(out=nxt.ap, in_=S32.ap, func=AF.Identity), reads=[S32.b], writes=[nxt.b])
                yield
            if d == 0:
                tmp = R["tmp"]
                cx.op("act", lambda e: e.activation(out=tmp.ap[:, 0:LB], in_=po.ap[:, 0:LB], func=AF.Identity), reads=[po.b], writes=[tmp.b])
                cx.dma("sp", self.OF[r0:r0 + 128, t0:t0 + LB], tmp.ap[:, 0:LB], reads=[tmp.b], lane=tmp.b.lane)
            else:
                of, oacc, tmp, sg, ost = R["of"], R["oacc"], R["tmp"], R["sg"], R["ost"]
                pn = self.psc
                cx.dma("sp", of.ap[:, 0:LB], self.OF[r0:r0 + 128, t0:t0 + LB], writes=[of.b], lane=of.b.lane)
                cx.dma("sp", sg.ap[:, 0:LB], self.SG[r0:r0 + 128, t0:t0 + LB], writes=[sg.b], lane=sg.b.lane)
                cx.op("dve", lambda e: e.tensor_tensor(out=oacc.ap[:, 0:LB], in0=po.ap[:, 0:LB], in1=of.ap[:, 0:LB], op=ALU.add),
                      reads=[po.b, of.b], writes=[oacc.b])
                cx.op("act", lambda e: e.activation(out=tmp.ap[:, 0:LB], in_=oacc.ap[:, 0:LB], func=AF.Square), reads=[oacc.b], writes=[tmp.b])
                cx.op("pe", lambda e: e.matmul(pn.ap[:, 0:LB], lhsT=self.t_ones.ap, rhs=tmp.ap[:, 0:LB], start=True, stop=True),
                      reads=[tmp.b, self.t_ones.b], writes=[pn.b])
                cx.op("act", lambda e: e.activation(out=tmp.ap[:, 0:LB], in_=pn.ap[:, 0:LB], func=AF.Sqrt, scale=1.0 / 128, bias=EPS),
                      reads=[pn.b], writes=[tmp.b])
                cx.op("dve", lambda e: e.reciprocal(out=tmp.ap[:, 0:LB], in_=tmp.ap[:, 0:LB]), reads=[tmp.b], writes=[tmp.b])
                cx.op("dve", lambda e: e.tensor_tensor(out=oacc.ap[:, 0:LB], in0=oacc.ap[:, 0:LB], in1=tmp.ap[:, 0:LB], op=ALU.mult),
                      reads=[oacc.b, tmp.b], writes=[oacc.b])
                cx.op("dve", lambda e: e.scalar_tensor_tensor(out=ost.ap[:, 0:LB], in0=oacc.ap[:, 0:LB], scalar=self.c_hgng.ap[:, 0:1],
                                                              in1=sg.ap[:, 0:LB], op0=ALU.mult, op1=ALU.mult),
                      reads=[oacc.b, sg.b, self.c_hgng.b], writes=[ost.b])
                cx.dma("sp", self.OHG[r0:r0 + 128, t0:t0 + LB], ost.ap[:, 0:LB], reads=[ost.b], lane=ost.b.lane)
            yield
        if kind == "prompt":
            cx.dma("sp", self.cst[pidx * 2 * H + d * H + h], S32.ap, reads=[S32.b], lane=S32.b.lane)
        yield

    def phase_V(self):
        cx, P = self.cx, self.P
        self.phase()
        SUM = self.tf(T, name="SUM")
        SQ = self.tf(T, name="SQ")
        UBN = 94 * 94
        ub = [self.tf(UBN + 32, lane=True, name=f"ub{i}") for i in range(2)]
        ubp = [self.tf(2 * 286, lane=True, name=f"ubp{i}") for i in range(2)]
        acc = [self.tf(T, lane=True, name=f"acc{i}") for i in range(2)]
        sqt = self.tf(T, name="sqt")
        for t_ in ub + ubp:
            cx.op("pool", lambda e, t_=t_: e.memset(t_.ap, 0.0), writes=[t_.b])
        cvw3 = self.c_cvw.ap.rearrange("p (c w) -> p c w", w=CW)
        cvb = self.c_cvb.ap
        for ct in range(16):
            u, up, ac = ub[ct % 2], ubp[ct % 2], acc[ct % 2]
            r0 = ct * 128
            u2 = u.ap[:, 0:UBN].rearrange("p (r c) -> p r c", c=94)
            usrc = self.U[r0:r0 + 128, 0:TS].rearrange("p (r c) -> p r c", c=64)
            for qq in range(4):
                cx.dma("sp", u2[:, 15 + qq * 16:15 + (qq + 1) * 16, 15:79], usrc[:, qq * 16:(qq + 1) * 16, :], writes=[u.b], lane=u.b.lane, cont=(qq > 0))
            up3 = up.ap.rearrange("p (s t) -> p s t", t=286)
            cx.dma("sp", up3[:, :, 15:271], self.U[r0:r0 + 128, TS:T].rearrange("p (s t) -> p s t", t=256), writes=[up.b], lane=up.b.lane)

            def srcs(j, u2=u2, up3=up3, ac=ac):
                s_row = u2[:, 15:79, j:j + 64]
                s_col = u2[:, j:j + 64, 15:79]
                s_a = ac.ap[:, 0:TS].rearrange("p (r c) -> p r c", c=64)
                p_s = up3[:, :, j:j + 256]
                p_a = ac.ap[:, TS:T].rearrange("p (s t) -> p s t", t=256)
                return s_row, s_col, s_a, p_s, p_a

            def f_conv(e, ct=ct, srcs=srcs):
                s_row, s_col, s_a, p_s, p_a = srcs(0)
                with e.If(self.is_row):
                    e.tensor_scalar(out=s_a, in0=s_row, scalar1=cvw3[:, ct, 0:1], scalar2=cvb[:, ct:ct + 1], op0=ALU.mult, op1=ALU.add)
                with e.Else():
                    e.tensor_scalar(out=s_a, in0=s_col, scalar1=cvw3[:, ct, 0:1], scalar2=cvb[:, ct:ct + 1], op0=ALU.mult, op1=ALU.add)
                return e.tensor_scalar(out=p_a, in0=p_s, scalar1=cvw3[:, ct, 0:1], scalar2=cvb[:, ct:ct + 1], op0=ALU.mult, op1=ALU.add)
            cx.op("dve", f_conv, reads=[u.b, up.b, self.c_cvw.b, self.c_cvb.b], writes=[ac.b])
            for j in range(1, CW):
                def f_tap(e, ct=ct, j=j, srcs=srcs):
                    s_row, s_col, s_a, p_s, p_a = srcs(j)
                    with e.If(self.is_row):
                        e.scalar_tensor_tensor(out=s_a, in0=s_row, scalar=cvw3[:, ct, j:j + 1], in1=s_a, op0=ALU.mult, op1=ALU.add)
                    with e.Else():
                        e.scalar_tensor_tensor(out=s_a, in0=s_col, scalar=cvw3[:, ct, j:j + 1], in1=s_a, op0=ALU.mult, op1=ALU.add)
                    return e.scalar_tensor_tensor(out=p_a, in0=p_s, scalar=cvw3[:, ct, j:j + 1], in1=p_a, op0=ALU.mult, op1=ALU.add)
                cx.op("dve", f_tap, reads=[u.b, up.b, ac.b, self.c_cvw.b], writes=[ac.b])
            cx.dma("sp", self.YC[r0:r0 + 128, :], ac.ap, reads=[ac.b], lane=ac.b.lane)
            cx.op("act", lambda e, ac=ac: e.activation(out=sqt.ap, in_=ac.ap, func=AF.Square), reads=[ac.b], writes=[sqt.b])
            if ct == 0:
                cx.op("pool", lambda e, ac=ac: e.tensor_copy(out=SUM.ap, in_=ac.ap), reads=[ac.b], writes=[SUM.b])
                cx.op("pool", lambda e: e.tensor_copy(out=SQ.ap, in_=sqt.ap), reads=[sqt.b], writes=[SQ.b])
            else:
                cx.op("pool", lambda e, ac=ac: e.tensor_tensor(out=SUM.ap, in0=SUM.ap, in1=ac.ap, op=ALU.add), reads=[ac.b, SUM.b], writes=[SUM.b])
                cx.op("pool", lambda e: e.tensor_tensor(out=SQ.ap, in0=SQ.ap, in1=sqt.ap, op=ALU.add), reads=[sqt.b, SQ.b], writes=[SQ.b])
        MEAN, RSTD = acc[0], acc[1]
        msq = sqt
        for blk in range(NBLK):
            sl = slice(blk * 512, (blk + 1) * 512)
            p1, p2 = P[(2 * blk) % 8], P[(2 * blk + 1) % 8]
            cx.op("pe", lambda e, sl=sl, p1=p1: e.matmul(p1.ap, lhsT=self.t_ones.ap, rhs=SUM.ap[:, sl], start=True, stop=True),
                  reads=[SUM.b, self.t_ones.b], writes=[p1.b])
            cx.op("pe", lambda e, sl=sl, p2=p2: e.matmul(p2.ap, lhsT=self.t_ones.ap, rhs=SQ.ap[:, sl], start=True, stop=True),
                  reads=[SQ.b, self.t_ones.b], writes=[p2.b])
            cx.op("act", lambda e, sl=sl, p1=p1: e.activation(out=MEAN.ap[:, sl], in_=p1.ap, func=AF.Identity, scale=1.0 / DH),
                  reads=[p1.b], writes=[MEAN.b])
            cx.op("dve", lambda e, sl=sl: e.tensor_tensor(out=msq.ap[:, sl], in0=MEAN.ap[:, sl], in1=MEAN.ap[:, sl], op=ALU.mult),
                  reads=[MEAN.b], writes=[msq.b])
            cx.op("dve", lambda e, sl=sl, p2=p2: e.scalar_tensor_tensor(out=RSTD.ap[:, sl], in0=p2.ap, scalar=1.0 / DH, in1=msq.ap[:, sl],
                                                                        op0=ALU.mult, op1=ALU.subtract),
                  reads=[p2.b, msq.b], writes=[RSTD.b])
            cx.op("act", lambda e, sl=sl: e.activation(out=RSTD.ap[:, sl], in_=RSTD.ap[:, sl], func=AF.Sqrt, bias=EPS),
                  reads=[RSTD.b], writes=[RSTD.b])
            cx.op("dve", lambda e, sl=sl: e.reciprocal(out=RSTD.ap[:, sl], in_=RSTD.ap[:, sl]), reads=[RSTD.b], writes=[RSTD.b])
        yr = Ring([Tile(ub[i].ap[:, 0:512], Buf(f"y{i}", cx.lane())) for i in range(2)])
        gr = Ring([Tile(ub[i].ap[:, 512:768].bitcast(BF16), Buf(f"g{i}", cx.lane())) for i in range(2)])
        orr = Ring([Tile(ub[i].ap[:, 768:1024].bitcast(BF16), Buf(f"o{i}", cx.lane())) for i in range(2)])
        cx.barrier()
        lng, lnb = self.c_lng, self.c_lnb
        for ct in range(16):
            r0 = ct * 128
            for blk in range(NBLK):
                sl = slice(blk * 512, (blk + 1) * 512)
                y, gg, oo = yr.next(), gr.next(), orr.next()
                cx.dma("sp", y.ap, self.YC[r0:r0 + 128, sl], writes=[y.b], lane=y.b.lane)
                cx.dma("sp", gg.ap, self.SGC[r0:r0 + 128, sl], writes=[gg.b], lane=gg.b.lane)

                cx.op("dve", lambda e, y=y, sl=sl: e.tensor_tensor(out=y.ap, in0=y.ap, in1=MEAN.ap[:, sl], op=ALU.subtract),
                      reads=[y.b, MEAN.b], writes=[y.b])
                cx.op("dve", lambda e, y=y, sl=sl: e.tensor_tensor(out=y.ap, in0=y.ap, in1=RSTD.ap[:, sl], op=ALU.mult),
                      reads=[y.b, RSTD.b], writes=[y.b])
                cx.op("act", lambda e, y=y, ct=ct: e.activation(out=y.ap, in_=y.ap, func=AF.Silu, scale=lng.ap[:, ct:ct + 1], bias=lnb.ap[:, ct:ct + 1]),
                      reads=[y.b, lng.b, lnb.b], writes=[y.b])
                cx.op("dve", lambda e, y=y, gg=gg, oo=oo: e.tensor_tensor(out=oo.ap, in0=y.ap, in1=gg.ap, op=ALU.mult),
                      reads=[y.b, gg.b], writes=[oo.b])
                cx.dma("sp", self.OCV[r0:r0 + 128, sl], oo.ap, reads=[oo.b], lane=oo.b.lane)

    def phase_C(self):
        cx, P = self.cx, self.P
        groups = [(i * 512, 512) for i in range(NBLK)]
        for (g0, gn) in groups:
            self.phase()
            nt = gn // 512
            ohg = self.tb(16 * 512, lane=True, name="ohg")
            ocv = self.tb(16 * 512, lane=True, name="ocv")
            mg = self.A.bf16(KC * 512)
            mg3 = mg.rearrange("p (k t) -> p k t", t=512)
            mgb = [Buf("mg0")]
            ohg3 = ohg.ap.rearrange("p (k t) -> p k t", t=512)
            ocv3 = ocv.ap.rearrange("p (k t) -> p k t", t=512)
            stg = Ring([self.tf(2048, lane=True, name=f"stg{i}") for i in range(2)])

            def load_cast(wt, srcw, n):
                for hh in range(n // 2048):
                    sg_ = stg.next()
                    cx.dma("sp", sg_.ap, srcw[:, hh * 2048:(hh + 1) * 2048], writes=[sg_.b], lane=sg_.b.lane)
                    cx.op("pool", lambda e, wt=wt, sg_=sg_, hh=hh: e.tensor_copy(out=wt.ap[:, hh * 2048:(hh + 1) * 2048], in_=sg_.ap),
                          reads=[sg_.b], writes=[wt.b])
            cx.dma("sp", ohg3[:, :, 0:gn], self.OHG.rearrange("(k p) t -> p k t", p=128)[:, :, g0:g0 + gn], writes=[ohg.b], lane=ohg.b.lane)
            cx.dma("sp", ocv3[:, :, 0:gn], self.OCV.rearrange("(k p) t -> p k t", p=128)[:, :, g0:g0 + gn], writes=[ocv.b], lane=ocv.b.lane)
            w16 = Ring([self.tb(16 * 128, name=f"wh{i}") for i in range(4)])
            w32 = Ring([self.tb(KC * 128, name=f"wo{i}") for i in range(2)])
            mring = Ring([self.tb(512, lane=True, name=f"m{i}") for i in range(4)])
            t1r = Ring([self.tf(512, name=f"t1{i}") for i in range(1)])
            t2r = Ring([self.tf(512, name=f"t2{i}") for i in range(1)])
            xr = Ring([self.tf(512, lane=True, name=f"xr{i}") for i in range(2)])
            pring = Ring(P)
            for nb in range(32):
                wh = w16.next()
                load_cast(wh, self.cwhp[self.wi(nb)], 2048)
                wc = w16.next()
                load_cast(wc, self.cwcp[self.wi(nb)], 2048)
                for ti in range(nt):
                    tsl = slice(ti * 512, (ti + 1) * 512)
                    tok = g0 + ti * 512
                    ph, pc = pring.next(), pring.next()

                    def f_h(e, wh=wh, ph=ph, tsl=tsl, ohg3=ohg3):
                        for kc in range(16):
                            last = e.matmul(ph.ap, lhsT=wh.ap[:, kc * 128:(kc + 1) * 128], rhs=ohg3[:, kc, tsl], start=(kc == 0), stop=(kc == 15))
                        return last
                    cx.op("pe", f_h, reads=[wh.b, ohg.b], writes=[ph.b])

                    def f_c(e, wc=wc, pc=pc, tsl=tsl, ocv3=ocv3):
                        for kc in range(16):
                            last = e.matmul(pc.ap, lhsT=wc.ap[:, kc * 128:(kc + 1) * 128], rhs=ocv3[:, kc, tsl], start=(kc == 0), stop=(kc == 15))
                        return last
                    cx.op("pe", f_c, reads=[wc.b, ocv.b], writes=[pc.b])
                    mh, mc = mring.next(), mring.next()
                    cx.dma("sp", mh.ap, self.MH[nb * 128:(nb + 1) * 128, tok:tok + 512], writes=[mh.b], lane=mh.b.lane)
                    cx.dma("sp", mc.ap, self.MC[nb * 128:(nb + 1) * 128, tok:tok + 512], writes=[mc.b], lane=mc.b.lane)
                    t1, t2 = t1r.next(), t2r.next()
                    cx.op("dve", lambda e, ph=ph, mh=mh, t1=t1: e.tensor_tensor(out=t1.ap, in0=ph.ap, in1=mh.ap, op=ALU.mult),
                          reads=[ph.b, mh.b], writes=[t1.b])
                    cx.op("dve", lambda e, pc=pc, mc=mc, t2=t2: e.tensor_tensor(out=t2.ap, in0=pc.ap, in1=mc.ap, op=ALU.mult),
                          reads=[pc.b, mc.b], writes=[t2.b])
                    cx.op("dve", lambda e, t1=t1, t2=t2, nb=nb, tsl=tsl, mg3=mg3: e.tensor_tensor(out=mg3[:, nb, tsl], in0=t1.ap, in1=t2.ap, op=ALU.add),
                          reads=[t1.b, t2.b], writes=[mgb[ti]])
            for eb_ in range(32):
                wo = w32.next()
                load_cast(wo, self.cwout[self.wi(eb_)], 4096)
                for ti in range(nt):
                    tsl = slice(ti * 512, (ti + 1) * 512)
                    tok = g0 + ti * 512
                    r = 0 if tok < TS else 1
                    po = pring.next()

                    def f_o(e, wo=wo, po=po, tsl=tsl, mg3=mg3):
                        for kc in range(KC):
                            last = e.matmul(po.ap, lhsT=wo.ap[:, kc * 128:(kc + 1) * 128], rhs=mg3[:, kc, tsl], start=(kc == 0), stop=(kc == KC - 1))
                        return last
                    cx.op("pe", f_o, reads=[wo.b, mgb[ti]], writes=[po.b])
                    xt = xr.next()
                    src = self.XT
                    cx.dma("sp", xt.ap, src[eb_ * 128:(eb_ + 1) * 128, tok:tok + 512], writes=[xt.b], lane=xt.b.lane)
                    cx.op("dve", lambda e, po=po, xt=xt, eb_=eb_, r=r: e.scalar_tensor_tensor(
                        out=xt.ap, in0=po.ap, scalar=self.cmod3[:, 64 + eb_, r:r + 1], in1=xt.ap, op0=ALU.mult, op1=ALU.add),
                        reads=[po.b, xt.b, self.c_mod.b], writes=[xt.b])
                    cx.dma("sp", self.XT[eb_ * 128:(eb_ + 1) * 128, tok:tok + 512], xt.ap, reads=[xt.b], lane=xt.b.lane)

    def phase_final(self):
        cx, P = self.cx, self.P
        self.phase()
        self.xbuf = Buf("xsrc")
        src = self.XT if self.nl > 0 else self.xT
        srcv = src.rearrange("(kc p) t -> p kc t", p=128)
        dstv = self.yT.rearrange("(kc p) t -> p kc t", p=128)
        xring = Ring([self.tf(PCS * 512, lane=True, name=f"x{i}") for i in range(2)])
        sqring = Ring([self.tf(PCS * 512, name=f"sq{i}") for i in range(2)])
        oring = Ring([self.tf(PCS * 512, lane=True, name=f"o{i}") for i in range(2)])
        rstd = self.tf(512, name="rstd")
        for blk in range(NBLK):
            tok0 = blk * 512
            self.norm_block(src, tok0, P[blk % 2], xring, sqring, rstd)
            for pc in range(NPC):
                xs = xring.next()
                cx.dma("sp", xs.ap.rearrange("p (k t) -> p k t", t=512), srcv[:, pc * PCS:(pc + 1) * PCS, tok0:tok0 + 512],
                       reads=[self.xbuf], writes=[xs.b], lane=xs.b.lane)
                ot = oring.next()
                cx.op("dve", lambda e, xs=xs: e.tensor_tensor(
                    out=xs.ap.rearrange("p (k t) -> p k t", t=512), in0=xs.ap.rearrange("p (k t) -> p k t", t=512),
                    in1=rstd.ap.rearrange("p (o t) -> p o t", o=1).to_broadcast([128, PCS, 512]), op=ALU.mult),
                    reads=[xs.b, rstd.b], writes=[xs.b])

                def f_y(e, xs=xs, ot=ot, pc=pc):
                    for j in range(PCS):
                        kc = pc * PCS + j
                        last = e.activation(out=ot.ap[:, j * 512:(j + 1) * 512], in_=xs.ap[:, j * 512:(j + 1) * 512],
                                            func=AF.Identity, scale=self.t_fing.ap[:, kc:kc + 1])
                    return last
                cx.op("act", f_y, reads=[xs.b, self.t_fing.b], writes=[ot.b])
                cx.dma("sp", dstv[:, pc * PCS:(pc + 1) * PCS, tok0:tok0 + 512], ot.ap.rearrange("p (k t) -> p k t", t=512),
                       reads=[ot.b], lane=ot.b.lane)


def _blocked(w):
    L, K, N = w.shape
    a = w.reshape(L, K // 128, 128, N // 128, 128)
    a = np.ascontiguousarray(a.transpose(0, 3, 2, 1, 4))
    return a.reshape(L, N // 128, 128, (K // 128) * 128)


def _pm(v, n):
    lead = v.shape[:-1]
    a = v.reshape(-1, n, 128)
    a = a.transpose(2, 0, 1)
    return np.ascontiguousarray(a.reshape(128, -1)), lead


_CACHE = {}


def _prepare(x_prompt, x_sample, state_hgrn, c, c_ctx, mod_w, mod_b, norm_g, w_in, hg_lb_logits, hg_norm_g,
             cv_dw_w, cv_dw_b, cv_ln_g, cv_ln_b, w_hproj, w_cproj, w_out, final_g, ncore=8):
    f = np.float32
    x_prompt = np.asarray(x_prompt, f)
    x_sample = np.asarray(x_sample, f)
    state_hgrn = np.asarray(state_hgrn, f)
    shared = {
        "modw": _blocked(np.asarray(mod_w, f)),
        "modb": _pm(np.asarray(mod_b, f), 96)[0],
        "normg": _pm(np.asarray(norm_g, f), KC)[0],
        "win": _blocked(np.asarray(w_in, f)).reshape(NL * 192, 128, KC * 128),
        "lbl": _pm(np.asarray(hg_lb_logits, f).reshape(NL, 2 * DH), 32)[0],
        "hgng": np.ascontiguousarray(np.asarray(hg_norm_g, f).T),
        "cvw": np.ascontiguousarray(np.asarray(cv_dw_w, f).reshape(NL, CW, 16, 128).transpose(3, 0, 2, 1).reshape(128, NL * 16 * CW)),
        "cvb": _pm(np.asarray(cv_dw_b, f), 16)[0],
        "lng": _pm(np.asarray(cv_ln_g, f), 16)[0],
        "lnb": _pm(np.asarray(cv_ln_b, f), 16)[0],
        "whp": _blocked(np.asarray(w_hproj, f)).reshape(NL * 32, 128, 16 * 128),
        "wcp": _blocked(np.asarray(w_cproj, f)).reshape(NL * 32, 128, 16 * 128),
        "wout": _blocked(np.asarray(w_out, f)).reshape(NL * 32, 128, KC * 128),
        "fing": _pm(np.asarray(final_g, f)[None], KC)[0],
        "ident": np.eye(128, dtype=f),
    }
    tri = np.tril(np.ones((32, 32), f))
    m_f = np.ascontiguousarray(tri.T)
    m_b = np.ascontiguousarray(tri)
    cm = np.stack([np.tile(m_f[:, None, :], (1, 16, 1)).reshape(32, 512), np.tile(m_b[:, None, :], (1, 16, 1)).reshape(32, 512)], axis=1)
    shared["cmask"] = np.ascontiguousarray(cm.reshape(32, 1024))
    c = np.asarray(c, f)
    c_ctx = np.asarray(c_ctx, f)
    in_maps = []
    for i in range(ncore):
        b = i // 2
        xs = np.concatenate([x_sample[b], x_prompt[2 * i], x_prompt[2 * i + 1]], axis=0)
        cc = np.stack([c[b], c_ctx], axis=0)
        cT = cc.reshape(2, KC, 128).transpose(2, 1, 0).reshape(128, KC * 2)
        m = dict(shared)
        m["xT"] = np.ascontiguousarray(xs.T)
        m["state0"] = np.ascontiguousarray(state_hgrn[b]).reshape(NL * 2 * H, 128, 128)
        m["cT"] = np.ascontiguousarray(cT)
        in_maps.append(m)
    return in_maps


def kernel(**inputs):
    f = np.float32
    ncore = 8
    in_maps = _prepare(**inputs, ncore=ncore)
    if "nc" not in _CACHE:
        _CACHE["nc"] = Prog().build()
    nc = _CACHE["nc"]
    res = run_bass_kernel_spmd(nc, in_maps, core_ids=list(range(ncore)))
    y_prompt = np.empty((16, TPR, D), f)
    y_sample = np.empty((4, TS, D), f)
    new_state = np.empty((16, NL, 2, H, 128, 128), f)
    for i in range(ncore):
        r = res.results[i]
        yT = r["yT"]
        if i % 2 == 0:
            y_sample[i // 2] = yT[:, 0:TS].T
        y_prompt[2 * i] = yT[:, TS:TS + TPR].T
        y_prompt[2 * i + 1] = yT[:, TS + TPR:T].T
        so = np.asarray(r["stout"]).reshape(NPR, NL, 2, H, 128, 128)
        new_state[2 * i] = so[0]
        new_state[2 * i + 1] = so[1]
    return (y_prompt, y_sample, new_state)
```
